# Optimizing a Trainium2 kernel written in Bass

```python
import math
import jax, jax.numpy as jnp
from jax import lax
import numpy as np

D_MODEL = 1024
BATCH = 4
SEQ = 8192
DEPTH = 4

N_MIXERS = 4
GROUP_W = D_MODEL // N_MIXERS
CONV_K = 31
S5_CH = 16
S5_GROUPS = GROUP_W // S5_CH
S5_STATE = 64
SHORT_K = 3
DA_HEADS = 4
DA_VDIM = GROUP_W // DA_HEADS
DA_QKDIM = DA_VDIM // 2
ROT_DIM = DA_QKDIM // 4
ROPE_THETA = 500000.0
Q_BLOCK = 128
FFN_HIDDEN = -(-8 * D_MODEL // (3 * 256)) * 256
IN_CONF = 2 * GROUP_W
IN_S5 = GROUP_W
IN_SC = 3 * GROUP_W
IN_DA = 3 * GROUP_W
OFF_S5 = IN_CONF
OFF_SC = OFF_S5 + IN_S5
OFF_DA = OFF_SC + IN_SC
IN_TOTAL = OFF_DA + IN_DA
DEEPNORM_ALPHA = (2.0 * DEPTH) ** 0.25
DEEPNORM_BETA = (8.0 * DEPTH) ** -0.25
LN_EPS = 1e-5

kernel_name = "hybrid_parallel_mixer_encoder"


def layer_norm(x, g, b):
    xf = x.astype(jnp.float32)
    mu = jnp.mean(xf, axis=-1, keepdims=True)
    xc = xf - mu
    var = jnp.mean(xc * xc, axis=-1, keepdims=True)
    return (xc * lax.rsqrt(var + LN_EPS) * g.astype(jnp.float32) + b.astype(jnp.float32)).astype(x.dtype)


def depthwise_conv(x, w):
    k = w.shape[0]
    pad = k // 2
    return lax.conv_general_dilated(
        x, w[:, None, :].astype(x.dtype), window_strides=(1,), padding=[(pad, pad)],
        dimension_numbers=('NWC', 'WIO', 'NWC'), feature_group_count=x.shape[-1])


def conformer_conv(h, dw_w, dw_b, ln_g, ln_b):
    a, g = jnp.split(h, 2, axis=-1)
    z = a * jax.nn.sigmoid(g)
    z = depthwise_conv(z, dw_w) + dw_b.astype(z.dtype)
    z = layer_norm(z, ln_g, ln_b)
    return jax.nn.silu(z)


def _ssm_combine(e1, e2):
    a1r, a1i, b1r, b1i = e1
    a2r, a2i, b2r, b2i = e2
    return (a2r * a1r - a2i * a1i,
            a2r * a1i + a2i * a1r,
            a2r * b1r - a2i * b1i + b2r,
            a2r * b1i + a2i * b1r + b2i)


def s5_mixer(u, a_re, a_im, log_step, b_re, b_im, c_re, c_im, d_skip, w_glu, b_glu):
    bsz, seq = u.shape[0], u.shape[1]
    uf = u.astype(jnp.float32).reshape(bsz, seq, S5_GROUPS, S5_CH)
    y = d_skip.astype(jnp.float32).reshape(S5_GROUPS, S5_CH) * uf
    for direction in (0, 1):
        lr = a_re[direction].astype(jnp.float32)
        li = a_im[direction].astype(jnp.float32)
        step = jnp.exp(log_step[direction].astype(jnp.float32))[:, None]
        mag = jnp.exp(lr * step)
        abr = mag * jnp.cos(li * step)
        abi = mag * jnp.sin(li * step)
        den = lr * lr + li * li
        pr = abr - 1.0
        fr = (pr * lr + abi * li) / den
        fi = (abi * lr - pr * li) / den
        br = b_re[direction].astype(jnp.float32)
        bi = b_im[direction].astype(jnp.float32)
        bbr = fr[..., None] * br - fi[..., None] * bi
        bbi = fr[..., None] * bi + fi[..., None] * br
        bur = jnp.einsum('bsgp,gnp->bsgn', uf, bbr)
        bui = jnp.einsum('bsgp,gnp->bsgn', uf, bbi)
        ar = jnp.broadcast_to(abr, bur.shape)
        ai = jnp.broadcast_to(abi, bur.shape)
        _, _, xr, xi = lax.associative_scan(_ssm_combine, (ar, ai, bur, bui), axis=1,
                                            reverse=(direction == 1))
        y = y + jnp.einsum('bsgn,gpn->bsgp', xr, c_re[direction].astype(jnp.float32)) \
              - jnp.einsum('bsgn,gpn->bsgp', xi, c_im[direction].astype(jnp.float32))
    y = jax.nn.gelu(y.reshape(bsz, seq, GROUP_W))
    y = y * jax.nn.sigmoid(y @ w_glu.astype(jnp.float32) + b_glu.astype(jnp.float32))
    return y.astype(u.dtype)


def short_gated_conv(h, conv_w):
    bg, cg, v = jnp.split(h, 3, axis=-1)
    return bg * depthwise_conv(cg * v, conv_w)


def partial_rope(t, cos, sin):
    half = ROT_DIM // 2
    c = cos[None, :, None, None, :].astype(t.dtype)
    s = sin[None, :, None, None, :].astype(t.dtype)
    x1 = t[..., :half]
    x2 = t[..., half:ROT_DIM]
    return jnp.concatenate([x1 * c - x2 * s, x2 * c + x1 * s, t[..., ROT_DIM:]], axis=-1)


def diff_attention(q, k, v, lam, subln_g, lam_init):
    bsz, seq = q.shape[0], q.shape[1]
    nblk = seq // Q_BLOCK
    qb = (q * (DA_QKDIM ** -0.5)).reshape(bsz, nblk, Q_BLOCK, DA_HEADS, 2, DA_QKDIM).swapaxes(0, 1)

    def block(q_blk):
        s = jnp.einsum('bqhcd,bkhcd->cbhqk', q_blk, k).astype(jnp.float32)
        p = jax.nn.softmax(s, axis=-1)
        a = p[0] - lam * p[1]
        return jnp.einsum('bhqk,bkhe->bqhe', a.astype(v.dtype), v)

    o = lax.map(block, qb).swapaxes(0, 1).reshape(bsz, seq, DA_HEADS, DA_VDIM)
    of = o.astype(jnp.float32)
    of = of * lax.rsqrt(jnp.mean(of * of, axis=-1, keepdims=True) + LN_EPS)
    of = of * subln_g.astype(jnp.float32) * (1.0 - lam_init)
    return of.reshape(bsz, seq, GROUP_W).astype(q.dtype)


def setup_inputs(seed: int = 0) -> dict:
    key = jax.random.key(seed)
    ks = jax.random.split(key, 32)
    f32 = jnp.float32
    L, G, N, P = DEPTH, S5_GROUPS, S5_STATE, S5_CH

    def nrm(k, shape, scale):
        return jax.random.normal(k, shape, f32) * scale

    x = jax.random.normal(ks[0], (BATCH, SEQ, D_MODEL), f32)
    w_in = nrm(ks[1], (L, D_MODEL, IN_TOTAL), D_MODEL ** -0.5)
    w_out = nrm(ks[2], (L, D_MODEL, D_MODEL), D_MODEL ** -0.5 * DEEPNORM_BETA)
    conf_dw_w = nrm(ks[3], (L, CONV_K, GROUP_W), CONV_K ** -0.5)
    conf_dw_b = nrm(ks[4], (L, GROUP_W), 0.02)
    conf_ln_g = 1.0 + nrm(ks[5], (L, GROUP_W), 0.02)
    conf_ln_b = nrm(ks[6], (L, GROUP_W), 0.02)
    s5_a_re = -0.5 + nrm(ks[7], (L, 2, G, N), 0.01)
    s5_a_im = math.pi * jnp.arange(N, dtype=f32) + nrm(ks[8], (L, 2, G, N), 0.01)
    s5_log_step = jax.random.uniform(ks[9], (L, 2, G), f32, math.log(1e-3), math.log(1e-1))
    s5_b_re = nrm(ks[10], (L, 2, G, N, P), (2.0 * P) ** -0.5)
    s5_b_im = nrm(ks[11], (L, 2, G, N, P), (2.0 * P) ** -0.5)
    s5_c_re = nrm(ks[12], (L, 2, G, P, N), (2.0 * N) ** -0.5)
    s5_c_im = nrm(ks[13], (L, 2, G, P, N), (2.0 * N) ** -0.5)
    s5_d = nrm(ks[14], (L, GROUP_W), 1.0)
    s5_w_glu = nrm(ks[15], (L, GROUP_W, GROUP_W), GROUP_W ** -0.5)
    s5_b_glu = nrm(ks[16], (L, GROUP_W), 0.02)
    sc_conv_w = nrm(ks[17], (L, SHORT_K, GROUP_W), SHORT_K ** -0.5)
    da_lq1 = nrm(ks[18], (L, DA_QKDIM), 0.1)
    da_lk1 = nrm(ks[19], (L, DA_QKDIM), 0.1)
    da_lq2 = nrm(ks[20], (L, DA_QKDIM), 0.1)
    da_lk2 = nrm(ks[21], (L, DA_QKDIM), 0.1)
    da_subln_g = 1.0 + nrm(ks[22], (L, DA_VDIM), 0.02)
    ln1_g = 1.0 + nrm(ks[23], (L, D_MODEL), 0.02)
    ln1_b = nrm(ks[24], (L, D_MODEL), 0.02)
    w_ffn1 = nrm(ks[25], (L, D_MODEL, FFN_HIDDEN), D_MODEL ** -0.5)
    w_ffn3 = nrm(ks[26], (L, D_MODEL, FFN_HIDDEN), D_MODEL ** -0.5)
    w_ffn2 = nrm(ks[27], (L, FFN_HIDDEN, D_MODEL), FFN_HIDDEN ** -0.5 * DEEPNORM_BETA)
    ln2_g = 1.0 + nrm(ks[28], (L, D_MODEL), 0.02)
    ln2_b = nrm(ks[29], (L, D_MODEL), 0.02)
    return {"x": x, "w_in": w_in, "w_out": w_out,
            "conf_dw_w": conf_dw_w, "conf_dw_b": conf_dw_b, "conf_ln_g": conf_ln_g, "conf_ln_b": conf_ln_b,
            "s5_a_re": s5_a_re, "s5_a_im": s5_a_im, "s5_log_step": s5_log_step,
            "s5_b_re": s5_b_re, "s5_b_im": s5_b_im, "s5_c_re": s5_c_re, "s5_c_im": s5_c_im,
            "s5_d": s5_d, "s5_w_glu": s5_w_glu, "s5_b_glu": s5_b_glu,
            "sc_conv_w": sc_conv_w,
            "da_lq1": da_lq1, "da_lk1": da_lk1, "da_lq2": da_lq2, "da_lk2": da_lk2, "da_subln_g": da_subln_g,
            "ln1_g": ln1_g, "ln1_b": ln1_b,
            "w_ffn1": w_ffn1, "w_ffn3": w_ffn3, "w_ffn2": w_ffn2,
            "ln2_g": ln2_g, "ln2_b": ln2_b}


def reference(x, w_in, w_out, conf_dw_w, conf_dw_b, conf_ln_g, conf_ln_b,
              s5_a_re, s5_a_im, s5_log_step, s5_b_re, s5_b_im, s5_c_re, s5_c_im,
              s5_d, s5_w_glu, s5_b_glu, sc_conv_w,
              da_lq1, da_lk1, da_lq2, da_lk2, da_subln_g,
              ln1_g, ln1_b, w_ffn1, w_ffn3, w_ffn2, ln2_g, ln2_b):
    bsz, seq = x.shape[0], x.shape[1]
    pos = jnp.arange(seq, dtype=jnp.float32)
    inv_freq = ROPE_THETA ** (-jnp.arange(0, ROT_DIM, 2, dtype=jnp.float32) / ROT_DIM)
    ang = pos[:, None] * inv_freq[None, :]
    cos, sin = jnp.cos(ang), jnp.sin(ang)

    for l in range(DEPTH):
        h = jnp.einsum('bsd,de->bse', x, w_in[l])
        y_a = conformer_conv(h[..., :OFF_S5], conf_dw_w[l], conf_dw_b[l], conf_ln_g[l], conf_ln_b[l])
        y_b = s5_mixer(h[..., OFF_S5:OFF_SC], s5_a_re[l], s5_a_im[l], s5_log_step[l],
                       s5_b_re[l], s5_b_im[l], s5_c_re[l], s5_c_im[l],
                       s5_d[l], s5_w_glu[l], s5_b_glu[l])
        y_c = short_gated_conv(h[..., OFF_SC:OFF_DA], sc_conv_w[l])
        h_da = h[..., OFF_DA:]
        q = h_da[..., :GROUP_W].reshape(bsz, seq, DA_HEADS, 2, DA_QKDIM)
        k = h_da[..., GROUP_W:2 * GROUP_W].reshape(bsz, seq, DA_HEADS, 2, DA_QKDIM)
        v = h_da[..., 2 * GROUP_W:].reshape(bsz, seq, DA_HEADS, DA_VDIM)
        q = partial_rope(q, cos, sin)
        k = partial_rope(k, cos, sin)
        lam_init = 0.8 - 0.6 * math.exp(-0.3 * l)
        lam = (jnp.exp(jnp.sum(da_lq1[l].astype(jnp.float32) * da_lk1[l].astype(jnp.float32)))
               - jnp.exp(jnp.sum(da_lq2[l].astype(jnp.float32) * da_lk2[l].astype(jnp.float32)))
               + lam_init)
        y_d = diff_attention(q, k, v, lam, da_subln_g[l], lam_init)
        mix = jnp.einsum('bse,ed->bsd', jnp.concatenate([y_a, y_b, y_c, y_d], axis=-1), w_out[l])
        x = layer_norm(DEEPNORM_ALPHA * x + mix, ln1_g[l], ln1_b[l])
        ff = jax.nn.silu(x @ w_ffn1[l]) * (x @ w_ffn3[l])
        x = layer_norm(DEEPNORM_ALPHA * x + ff @ w_ffn2[l], ln2_g[l], ln2_b[l])
    return x
```

```python
import math
from contextlib import ExitStack
import numpy as np
import concourse.bass as bass
import concourse.mybir as mybir
from concourse.bass_utils import run_bass_kernel_spmd

F32 = mybir.dt.float32
BF16 = mybir.dt.bfloat16
ALU = mybir.AluOpType
AF = mybir.ActivationFunctionType

D = 1024
GW = 256
FF = 2816
NFC = FF // 128
CONV_K = 31
ALPHA = (2.0 * 4) ** 0.25
LN_EPS = 1e-5
ROPE_THETA = 500000.0
NDS = 24
PAIRS = [[0, 1], [2, 3], [4, 5], [6, 7]]


class Prog:
    def __init__(self, nc):
        self.nc = nc
        self.names = ['pe', 'dve', 'act', 'pool', 'sp']
        self.stream = {k: [] for k in self.names}
        self.sems = {}
        for k in ('pe', 'dve', 'act', 'pool'):
            self.sems[('c', k)] = nc.alloc_semaphore('c_' + k)
        for i in range(NDS):
            self.sems[('d', i)] = nc.alloc_semaphore('d%d' % i)
        self.tick = {k: 0 for k in ('pe', 'dve', 'act', 'pool')}
        self.dcount = [0] * NDS
        self.dnext = 0
        self.ncc = 0
        self.waited = {k: {} for k in self.names}
        self.lastw = {}
        self.readers = {}

    def _deps(self, e, reads, writes):
        need = {}
        toks = []
        for k in reads:
            if k in self.lastw:
                toks.append(self.lastw[k])
        for k in writes:
            if k in self.lastw:
                toks.append(self.lastw[k])
            toks += self.readers.get(k, [])
        for sk, v in toks:
            if need.get(sk, 0) < v:
                need[sk] = v
        out = []
        for sk, v in need.items():
            if self.waited[e].get(sk, 0) >= v:
                continue
            self.waited[e][sk] = v
            out.append((sk, v))
        return out

    def _commit(self, tok, reads, writes):
        for k in writes:
            self.lastw[k] = tok
            self.readers[k] = []
        for k in reads:
            self.readers.setdefault(k, []).append(tok)

    def op(self, e, fn, reads=(), writes=()):
        waits = self._deps(e, reads, writes)
        if e == 'pe':
            waits = [w for w in waits if w[0] != ('c', 'pe')]
        self.tick[e] += 1
        tok = (('c', e), self.tick[e])
        self.stream[e].append((fn, waits, ('c', e), 1))
        self._commit(tok, reads, writes)

    def dma(self, q, out, in_, reads=(), writes=()):
        waits = self._deps(q, reads, writes)
        i = self.dnext
        self.dnext = (i + 1) % NDS
        prev = self.dcount[i]
        if prev > 0 and self.waited[q].get(('d', i), 0) < prev:
            waits.append((('d', i), prev))
            self.waited[q][('d', i)] = prev
        self.dcount[i] += 16
        tok = (('d', i), self.dcount[i])
        self.stream[q].append((lambda eng: eng.dma_start(out=out, in_=in_), waits, ('d', i), 16))
        self._commit(tok, reads, writes)

    def allgather(self, src, dst, reads, writes):
        waits = self._deps('pool', reads, writes)
        key = ('cc', self.ncc)
        self.ncc += 1
        self.sems[key] = self.nc.alloc_semaphore('cc%d' % key[1])
        tok = (key, 1)

        def fn(eng):
            return eng.collective_compute("AllGather", ALU.bypass, replica_groups=PAIRS,
                                          ins=[src], outs=[dst])
        self.stream['pool'].append((fn, waits, key, 1))
        self._commit(tok, reads, writes)

    def emit(self):
        nc = self.nc
        streams = self.stream
        self.stream = {k: [] for k in self.names}
        sems = self.sems

        def run(eng, lst):
            for fn, waits, sk, inc in lst:
                for wk, v in waits:
                    eng.wait_ge(sems[wk], v)
                ins = fn(eng)
                ins.then_inc(sems[sk], inc)

        drain = [(('d', i), self.dcount[i]) for i in range(NDS) if self.dcount[i] > 0]
        for wk, v in drain:
            self.waited['sp'][wk] = max(self.waited['sp'].get(wk, 0), v)

        with nc.Block() as block:
            @block.tensor
            def _(e):
                run(e, streams['pe'])

            @block.vector
            def _(e):
                run(e, streams['dve'])

            @block.scalar
            def _(e):
                run(e, streams['act'])

            @block.gpsimd
            def _(e):
                run(e, streams['pool'])

            @block.sync
            def _(e):
                run(e, streams['sp'])
                for wk, v in drain:
                    e.wait_ge(sems[wk], v)

    def final_wait(self, keys):
        waits = self._deps('sp', keys, ())
        nc = self.nc
        sems = self.sems
        with nc.Block() as block:
            @block.sync
            def _(e):
                for wk, v in waits:
                    e.wait_ge(sems[wk], v)


def build(NC, L, debug=False):
    NT = 8 * NC
    S2 = 2 * NT
    CW = min(NC, 512)
    NCT = NC // CW
    TW = 512
    NTT = NT // TW
    KC = S2 // 128
    NLEV = int(round(math.log2(2 * NC)))
    nc = bass.Bass("TRN2", target_bir_lowering=False)

    def din(name, shape, dt=F32):
        return nc.dram_tensor(name, list(shape), dt, kind="ExternalInput")

    xT = din("xT", [D, NT])
    w_in = din("w_in", [L, D, 2304])
    w_out = din("w_out", [L, D, D])
    w1 = din("w1", [L, D, FF])
    w3 = din("w3", [L, D, FF])
    w2 = din("w2", [L, FF, D])
    cosT = din("cosT", [128, NT])
    sinT = din("sinT", [128, NT])
    masks = din("masks", [128, 4])
    ident = din("ident", [128, 128])
    jabs = din("jabs", [128, 128])
    sgn = din("sgn", [128, 1])
    cdw = din("cdw", [L, 128, CONV_K])
    cdb = din("cdb", [L, 128, 1])
    scw = din("scw", [L, 128, 3])
    clng = din("clng", [L, 128, 2])
    clnb = din("clnb", [L, 128, 2])
    s5lr = din("s5lr", [L, 128, 16])
    s5li = din("s5li", [L, 128, 16])
    s5ls = din("s5ls", [L, 128, 16])
    s5b0 = din("s5b0", [L, 128, 16, 16])
    s5b1 = din("s5b1", [L, 128, 16, 16])
    s5c0 = din("s5c0", [L, 128, 16, 16])
    s5d = din("s5d", [L, 128, 8])
    wglu = din("wglu", [L, GW, GW])
    bglu = din("bglu", [L, 128, 2])
    lqk = din("lqk", [L, 128, 4, 32])
    sublng = din("sublng", [L, 128, 64])
    ln1g = din("ln1g", [L, 128, 8])
    ln1b = din("ln1b", [L, 128, 8])
    ln2g = din("ln2g", [L, 128, 8])
    ln2b = din("ln2b", [L, 128, 8])
    yT = nc.dram_tensor("yT", [D, NT], F32, kind="ExternalOutput")

    if debug:
        dbg_loc = nc.dram_tensor("dbg_loc", [768, NT], BF16, kind="ExternalOutput")
        dbg_res = nc.dram_tensor("dbg_res", [512, 2 * NT], BF16, kind="ExternalOutput")
        dbg_bgd = nc.dram_tensor("dbg_bgd", [256, NT], BF16, kind="ExternalOutput")
    xres = nc.dram_tensor("xres", [D, NT], F32)
    loc = nc.dram_tensor("loc", [768, NT], BF16)
    snd_p = [nc.dram_tensor("snd%d" % i, [256, NT], BF16) for i in range(3)]
    gat_p = [nc.dram_tensor("gat%d" % i, [512, NT], BF16) for i in range(3)]
    bgd = nc.dram_tensor("bgd", [256, NT], BF16)
    if S2 * 128 * 2 <= 2 * 1024 * 1024:
        RSPLIT = 1
    else:
        RSPLIT = (S2 * 128 * 2) // (2 * 1024 * 1024)
    RW = S2 // RSPLIT
    res_p = [[nc.dram_tensor("res%d_%d" % (i, j), [128, RW], BF16) for j in range(RSPLIT)] for i in range(4)]
    gat2_p = [[nc.dram_tensor("gat2%d_%d" % (i, j), [256, RW], BF16) for j in range(RSPLIT)] for i in range(4)]

    def snd_ap(row0, n):
        return snd_p[row0 // 256].ap()[row0 % 256:row0 % 256 + n, :]

    def gat_ap(r, row0, n):
        o = r * 256 + row0 % 256
        return gat_p[row0 // 256].ap()[o:o + n, :]

    def res_ap(mixi, prow0, n, col0, w):
        j = col0 // RW
        assert (col0 + w - 1) // RW == j
        return res_p[mixi][j].ap()[prow0:prow0 + n, col0 - j * RW:col0 - j * RW + w]

    def gat2_ap(mixi, rank, col0, w):
        j = col0 // RW
        assert (col0 + w - 1) // RW == j
        return gat2_p[mixi][j].ap()[rank * 128:rank * 128 + 128, col0 - j * RW:col0 - j * RW + w]

    P = Prog(nc)
    lam_inits = [0.8 - 0.6 * math.exp(-0.3 * l) for l in range(L)]

    with ExitStack() as top:
        def sb(name, shape, dt):
            return top.enter_context(nc.sbuf_tensor(name, list(shape), dt))
        ps = [top.enter_context(nc.psum_tensor("ps%d" % i, [128, 512], F32)) for i in range(8)]
        psk = [('ps', i) for i in range(8)]
        xb = sb("xb", [128, 8, NT], BF16)
        msk = sb("msk", [128, 4], F32)
        idf = sb("idf", [128, 128], F32)
        idb = sb("idb", [128, 128], BF16)
        jab = sb("jab", [128, 128], F32)
        sg1 = sb("sg1", [128, 1], F32)
        nsg = sb("nsg", [128, 1], F32)
        onesf = sb("onesf", [128, 128], F32)

        P.dma('sp', msk[:], masks.ap(), writes=['msk'])
        P.dma('sp', idf[:], ident.ap(), writes=['idf'])
        P.dma('sp', jab[:], jabs.ap(), writes=['jab'])
        P.dma('sp', sg1[:], sgn.ap(), writes=['sg1'])
        P.op('dve', lambda e: e.tensor_copy(out=idb[:], in_=idf[:]), reads=['idf'], writes=['idb'])
        P.op('dve', lambda e: e.tensor_scalar(out=nsg[:], in0=sg1[:], scalar1=-1.0, scalar2=None, op0=ALU.mult),
             reads=['sg1'], writes=['nsg'])
        P.op('pool', lambda e: e.memset(onesf[:], 1.0), writes=['onesf'])
        P.dma('sp', xres.ap(), xT.ap(), writes=['xres'])
        for dc in range(8):
            P.dma('pool', xb[:, dc, :], xT.ap()[dc * 128:(dc + 1) * 128, :], writes=[('xb', dc)])
        P.emit()

        bi = [0]

        def next_ps():
            i = bi[0]
            bi[0] = (i + 1) % 8
            return i

        def mm_group(pi, pairs, M=128, N=512, extra_reads=(), pbase=0):
            n = len(pairs)

            def fn(e):
                ins = None
                for j, (a, b) in enumerate(pairs):
                    ins = e.matmul(ps[pi][pbase:pbase + M, 0:N], lhsT=a, rhs=b, start=(j == 0), stop=(j == n - 1))
                return ins
            P.op('pe', fn, reads=list(extra_reads), writes=[psk[pi]])

        for l in range(L):
            lam_init = lam_inits[l]
            with ExitStack() as ph:
                def sa(name, shape, dt):
                    return ph.enter_context(nc.sbuf_tensor(name + '_L%d' % l, list(shape), dt))
                wb = [sa("wbA%d" % i, [128, 8, 4, 32], BF16) for i in range(2)]
                wp = [sa("wpA%d" % i, [128, 8, 4, 32], BF16) for i in range(2)]
                gate = sa("gateA", [128, NT], F32)
                cs = sa("cosA", [128, NT], F32)
                sn = sa("sinA", [128, NT], F32)
                ot = [sa("otA%d" % i, [128, TW], BF16) for i in range(3)]
                tq = [sa("tqA%d" % i, [128, TW], F32) for i in range(2)]
                vt = [sa("vtA%d" % i, [128, 4, 128], BF16) for i in range(2)]
                P.dma('sp', cs[:], cosT.ap(), writes=['cs'])
                P.dma('sp', sn[:], sinT.ap(), writes=['sn'])
                for i in range(2):
                    P.op('pool', (lambda i: lambda e: e.memset(wp[i][:], 0.0))(i), writes=[('wp', i)])
                wi = [0]
                oi = [0]
                w_in_l = w_in.ap()[l].rearrange("(dc p) e -> p dc e", p=128)

                def load_w(col0):
                    i = wi[0]
                    wi[0] ^= 1
                    P.dma('pool', wb[i][:].rearrange("p a b c -> p a (b c)"), w_in_l[:, :, col0:col0 + 128],
                          writes=[('wb', i)])
                    return i

                def proj_tile(i, tt, wt=None, wkey='wb'):
                    wt = wb[i] if wt is None else wt
                    wv = wt[:].rearrange("p a b c -> p a (b c)")
                    pi = next_ps()
                    mm_group(pi, [(wv[:, dc, :], xb[:, dc, tt * TW:(tt + 1) * TW]) for dc in range(8)],
                             extra_reads=[(wkey, i)] + [('xb', dc) for dc in range(8)])
                    return pi

                def dst_rows(half, row0):
                    return loc.ap()[row0:row0 + 128, :] if half == 0 else snd_ap(row0, 128)

                def store_tile(src_i, dst_ap, dkey):
                    P.dma('sp', dst_ap, ot[src_i][:], reads=[('ot', src_i)], writes=[dkey])

                def next_ot():
                    i = oi[0]
                    oi[0] = (i + 1) % 3
                    return i

                def do_half(half):
                    dk = 'loc' if half == 0 else 'snd'
                    hc = half * 128
                    for (cg0, cv0, row0, gfunc) in ((256 + hc, 0 + hc, 0, 'sig'), (1024 + hc, 1280 + hc, 256, 'copy')):
                        i = load_w(cg0)
                        for tt in range(NTT):
                            pi = proj_tile(i, tt)
                            sl = slice(tt * TW, (tt + 1) * TW)
                            if gfunc == 'sig':
                                P.op('act', (lambda pi, sl: lambda e: e.activation(out=gate[:, sl], in_=ps[pi][:, :], func=AF.Sigmoid))(pi, sl),
                                     reads=[psk[pi]], writes=[('gate', tt)])
                            else:
                                P.op('act', (lambda pi, sl: lambda e: e.activation(out=gate[:, sl], in_=ps[pi][:, :], func=AF.Copy))(pi, sl),
                                     reads=[psk[pi]], writes=[('gate', tt)])
                        i = load_w(cv0)
                        for tt in range(NTT):
                            pi = proj_tile(i, tt)
                            sl = slice(tt * TW, (tt + 1) * TW)
                            o = next_ot()
                            P.op('dve', (lambda pi, sl, o: lambda e: e.tensor_tensor(out=ot[o][:], in0=ps[pi][:, :], in1=gate[:, sl], op=ALU.mult))(pi, sl, o),
                                 reads=[psk[pi], ('gate', tt)], writes=[('ot', o)])
                            store_tile(o, dst_rows(half, row0)[:, sl], (dk, row0, tt))
                    i = load_w(512 + hc)
                    for tt in range(NTT):
                        pi = proj_tile(i, tt)
                        sl = slice(tt * TW, (tt + 1) * TW)
                        o = next_ot()
                        P.op('act', (lambda pi, o: lambda e: e.activation(out=ot[o][:], in_=ps[pi][:, :], func=AF.Copy))(pi, o),
                             reads=[psk[pi]], writes=[('ot', o)])
                        store_tile(o, dst_rows(half, 128)[:, sl], (dk, 128, tt))
                    i = load_w(768 + hc)
                    for tt in range(NTT):
                        pi = proj_tile(i, tt)
                        sl = slice(tt * TW, (tt + 1) * TW)
                        o = next_ot()
                        P.op('act', (lambda pi, o: lambda e: e.activation(out=ot[o][:], in_=ps[pi][:, :], func=AF.Copy))(pi, o),
                             reads=[psk[pi]], writes=[('ot', o)])
                        store_tile(o, bgd.ap()[hc:hc + 128, sl], ('bgd', half, tt))
                    for (c0, row0) in ((1536 + hc, 384), (1792 + hc, 512)):
                        i = load_w(c0)
                        P.op('pool', (lambda i: lambda e: e.tensor_scalar(out=wp[i][:, :, :, 0:4], in0=wb[i][:, :, :, 4:8], scalar1=-1.0, scalar2=None, op0=ALU.mult))(i),
                             reads=[('wb', i)], writes=[('wp', i)])
                        P.op('pool', (lambda i: lambda e: e.tensor_copy(out=wp[i][:, :, :, 4:8], in_=wb[i][:, :, :, 0:4]))(i),
                             reads=[('wb', i)], writes=[('wp', i)])
                        for tt in range(NTT):
                            sl = slice(tt * TW, (tt + 1) * TW)
                            pi = proj_tile(i, tt)
                            pj = proj_tile(i, tt, wt=wp[i], wkey='wp')
                            t = tt % 2
                            o = next_ot()
                            P.op('dve', (lambda pi, sl, t: lambda e: e.tensor_tensor(out=tq[t][:], in0=ps[pi][:, :], in1=cs[:, sl], op=ALU.mult))(pi, sl, t),
                                 reads=[psk[pi], 'cs'], writes=[('tq', t)])
                            P.op('dve', (lambda pj, sl, t: lambda e: e.tensor_tensor(out=ps[pj][:, :], in0=ps[pj][:, :], in1=sn[:, sl], op=ALU.mult))(pj, sl, t),
                                 reads=[psk[pj], 'sn'], writes=[psk[pj]])
                            P.op('dve', (lambda pj, t, o: lambda e: e.tensor_tensor(out=ot[o][:], in0=ps[pj][:, :], in1=tq[t][:], op=ALU.add))(pj, t, o),
                                 reads=[psk[pj], ('tq', t)], writes=[('ot', o)])
                            store_tile(o, dst_rows(half, row0)[:, sl], (dk, row0, tt))
                    i = load_w(2048 + hc)
                    wv = wb[i][:].rearrange("p a b c -> p a (b c)")
                    vdst = dst_rows(half, 640).rearrange("a (b c) -> (a b) c", c=128).rearrange("(i p) e -> p i e", p=128)
                    for t4 in range(NT // 512):
                        pi = next_ps()
                        for q4 in range(4):
                            tk = t4 * 4 + q4

                            def fn(e, tk=tk, q4=q4, pi=pi, wv=wv):
                                ins = None
                                for dc in range(8):
                                    ins = e.matmul(ps[pi][:, q4 * 128:(q4 + 1) * 128], lhsT=xb[:, dc, tk * 128:(tk + 1) * 128],
                                                   rhs=wv[:, dc, :], start=(dc == 0), stop=(dc == 7))
                                return ins
                            P.op('pe', fn, reads=[('wb', i)] + [('xb', dc) for dc in range(8)], writes=[psk[pi]])
                        v_i = t4 % 2
                        P.op('act', (lambda pi, v_i: lambda e: e.activation(out=vt[v_i][:].rearrange("p a b -> p (a b)"), in_=ps[pi][:, :], func=AF.Copy))(pi, v_i),
                             reads=[psk[pi]], writes=[('vt', v_i)])
                        P.dma('sp', vdst[:, t4 * 4:(t4 + 1) * 4, :], vt[v_i][:], reads=[('vt', v_i)], writes=[(dk, 640, t4)])
                for half in range(2):
                    do_half(half)
                P.emit()
            snd_keys = [('snd', r0, tt) for r0 in (0, 128, 256, 384, 512) for tt in range(NTT)] + [('snd', 640, t4) for t4 in range(NT // 512)]
            for pc in range(3):
                P.allgather(snd_p[pc].ap().opt(), gat_p[pc].ap().opt(), reads=snd_keys, writes=[('gat', pc)])
            GK = [('gat', pc) for pc in range(3)]
            loc_keys_all = [('loc', r0, tt) for r0 in (0, 128, 256, 384, 512) for tt in range(NTT)] + [('loc', 640, t4) for t4 in range(NT // 512)]

            with ExitStack() as ph:
                def sa(name, shape, dt):
                    return ph.enter_context(nc.sbuf_tensor(name + '_L%d' % l, list(shape), dt))
                tA = sa("tA", [128, NT], BF16)
                tB = sa("tB", [128, NT], BF16)
                zb = sa("zb", [128, 8, 2 * NC + 4], BF16)
                dg = sa("dg", [128, CONV_K, 128], BF16)
                cw = sa("cw", [128, CONV_K], F32)
                cb = sa("cb", [128, 1], F32)
                sw = sa("sw", [128, 3], F32)
                ob = [sa("obB%d" % i, [128, CW], BF16) for i in range(2)]

                def select_into(row0, r, out_ap, view, nparts=128, src_rows=None, eng2='dve'):
                    rows = slice(row0, row0 + nparts)
                    P.dma('sp', tA[0:nparts, :], loc.ap()[rows, :], reads=loc_keys_all, writes=['tA'])
                    P.dma('sp', tB[0:nparts, :], gat_ap(r, row0, nparts), reads=GK, writes=['tB'])
                    P.op('pool', lambda e: e.tensor_scalar(out=out_ap, in0=view(tB[0:nparts, :]), scalar1=msk[0:nparts, 2 + r:3 + r], scalar2=None, op0=ALU.mult),
                         reads=['tB', 'msk'], writes=['selout'])
                    P.op('dve', lambda e: e.scalar_tensor_tensor(out=out_ap, in0=view(tA[0:nparts, :]), scalar=msk[0:nparts, r:r + 1], in1=out_ap, op0=ALU.mult, op1=ALU.add),
                         reads=['tA', 'msk', 'selout'], writes=['selout'])

                def conv_pass(row0, ntap, wcol, bias, res_row0):
                    P.op('pool', lambda e: e.memset(zb[:], 0.0), reads=['selout'], writes=['selout'])
                    for r in range(2):
                        select_into(row0, r, zb[:, :, 2 + r * NC:2 + (r + 1) * NC], lambda a: a.rearrange("p (t c) -> p t c", t=8))
                    for j in range(ntap):
                        P.op('dve', (lambda j: lambda e: e.tensor_scalar(out=dg[:, j, :], in0=idf[:], scalar1=wcol[:, j:j + 1], scalar2=None, op0=ALU.mult))(j),
                             reads=['idf', 'cwts'], writes=[('dg', j)])
                    half_k = ntap // 2
                    oi2 = 0
                    for r in range(2):
                        for tau in range(8):
                            for ct in range(NCT):
                                C0 = r * NC + ct * CW
                                pairs = []
                                for j in range(ntap):
                                    dl = j - half_k
                                    qq, tp = divmod(tau + dl, 8)
                                    pairs.append((dg[:, j, :], zb[:, tp, 2 + C0 + qq:2 + C0 + qq + CW]))
                                pi = next_ps()
                                mm_group(pi, pairs, N=CW, extra_reads=['selout'] + [('dg', j) for j in range(ntap)])
                                o = oi2
                                oi2 ^= 1
                                if bias is not None:
                                    P.op('act', (lambda pi, o: lambda e: e.activation(out=ob[o][:], in_=ps[pi][:, 0:CW], func=AF.Identity, bias=bias[:, 0:1]))(pi, o),
                                         reads=[psk[pi], 'cwts'], writes=[('ob', o)])
                                else:
                                    P.op('act', (lambda pi, o: lambda e: e.activation(out=ob[o][:], in_=ps[pi][:, 0:CW], func=AF.Copy))(pi, o),
                                         reads=[psk[pi]], writes=[('ob', o)])
                                col = r * NT + tau * NC + ct * CW
                                P.dma('sp', res_ap(res_row0 // 128, 0, 128, col, CW), ob[o][:], reads=[('ob', o)], writes=[('res', res_row0, r, tau, ct)])

                P.dma('sp', cw[:], cdw.ap()[l], writes=['cwts'])
                P.dma('sp', cb[:], cdb.ap()[l], writes=['cwts'])
                P.dma('sp', sw[:], scw.ap()[l], writes=['cwts'])
                conv_pass(0, CONV_K, cw, cb, 0)
                conv_pass(256, 3, sw, None, 256)
                P.emit()

            with ExitStack() as ph:
                def sa(name, shape, dt):
                    return ph.enter_context(nc.sbuf_tensor(name + '_L%d' % l, list(shape), dt))
                NJ = 9 + NLEV
                ms = list(range(9)) + [8 * 2 ** k for k in range(1, NLEV)] + [0]
                ms = ms[:NJ]
                tA = sa("tA5", [128, NT], BF16)
                tB = sa("tB5", [128, NT], BF16)
                ufm = sa("ufm", [128, 8, 2 * NC], BF16)
                ub = sa("ub", [128, 8, 2 * NC], BF16)
                lr = sa("lr", [128, 16], F32)
                li = sa("li", [128, 16], F32)
                st = sa("st", [128, 16], F32)
                lrs = sa("lrs", [128, 16], F32)
                phi = sa("phi", [128, 16], F32)
                mag = sa("mag", [128, 16, NJ], F32)
                ang = sa("ang", [128, 16, NJ], F32)
                an2 = sa("an2", [128, 16, NJ], F32)
                kk = sa("kk", [128, 16, NJ], F32)
                cr = sa("cr", [128, 16, NJ], F32)
                ci = sa("ci", [128, 16, NJ], F32)
                cin = sa("cin", [128, 16, NJ], F32)
                fr = sa("fr", [128, 16], F32)
                fi = sa("fi", [128, 16], F32)
                t1 = sa("t1", [128, 16], F32)
                t2 = sa("t2", [128, 16], F32)
                den = sa("den", [128, 16], F32)
                b0 = sa("b0", [128, 16, 16], F32)
                b1 = sa("b1", [128, 16, 16], F32)
                c0 = sa("c0", [128, 16, 16], F32)
                bst = sa("bst", [128, 16, 16], F32)
                dsk = sa("dsk", [128, 8], F32)
                bpad = sa("bpad", [128, 2, 240], F32)
                epad = sa("epad", [128, 2, 240], F32)
                ebuf = sa("ebuf", [128, 2, 9, 16], F32)
                rm = sa("rm", [128, 2, 9, 128], F32)
                qm = sa("qm", [128, 2, 8, 128], F32)
                qlev = sa("qlev", [128, 2, NLEV, 128], F32)
                wS = sa("wS", [128, 2, 128], BF16)
                wY = sa("wY", [128, 2, 128], BF16)
                wK = sa("wK", [128, 128], BF16)
                X = [sa("X%d" % d, [128, 2 * NC + 1], F32) for d in range(2)]
                Xb = [sa("Xb%d" % d, [128, 2 * NC + 1], BF16) for d in range(2)]
                yt = [sa("yt%d" % i, [128, 2 * NC], BF16) for i in range(2)]

                def select5(row0, r, out_ap, view):
                    P.dma('sp', tA[:, :], loc.ap()[row0:row0 + 128, :], reads=loc_keys_all, writes=['tA'])
                    P.dma('sp', tB[:, :], gat_ap(r, row0, 128), reads=GK, writes=['tB'])
                    P.op('pool', lambda e: e.tensor_scalar(out=out_ap, in0=view(tB[:, :]), scalar1=msk[:, 2 + r:3 + r], scalar2=None, op0=ALU.mult),
                         reads=['tB', 'msk'], writes=['ufm'])
                    P.op('dve', lambda e: e.scalar_tensor_tensor(out=out_ap, in0=view(tA[:, :]), scalar=msk[:, r:r + 1], in1=out_ap, op0=ALU.mult, op1=ALU.add),
                         reads=['tA', 'msk', 'ufm'], writes=['ufm'])
                for r in range(2):
                    select5(128, r, ufm[:, :, r * NC:(r + 1) * NC], lambda a: a.rearrange("p (t c) -> p t c", t=8))
                for g in range(8):
                    for tau in range(8):
                        P.dma('sp', ub[16 * tau:16 * tau + 16, g, :], ufm[16 * g:16 * g + 16, tau, :], reads=['ufm'], writes=[('ub', g)])
                for (t_, src) in ((lr, s5lr), (li, s5li), (st, s5ls)):
                    P.dma('sp', t_[:], src.ap()[l], writes=['s5in'])
                P.dma('sp', b0[:], s5b0.ap()[l], writes=['s5in'])
                P.dma('sp', b1[:], s5b1.ap()[l], writes=['s5in'])
                P.dma('sp', c0[:], s5c0.ap()[l], writes=['s5in'])
                P.dma('sp', dsk[:], s5d.ap()[l], writes=['s5in'])
                V = 's5sc'
                P.op('act', lambda e: e.activation(out=st[:], in_=st[:], func=AF.Exp), reads=['s5in'], writes=[V])
                P.op('dve', lambda e: e.tensor_tensor(out=lrs[:], in0=lr[:], in1=st[:], op=ALU.mult), reads=[V, 's5in'], writes=[V])
                P.op('dve', lambda e: e.tensor_tensor(out=phi[:], in0=li[:], in1=st[:], op=ALU.mult), reads=[V, 's5in'], writes=[V])
                for j, m in enumerate(ms):
                    P.op('dve', (lambda j, m: lambda e: e.tensor_scalar(out=mag[:, :, j], in0=lrs[:], scalar1=float(m), scalar2=None, op0=ALU.mult))(j, m), reads=[V], writes=[V])
                    P.op('dve', (lambda j, m: lambda e: e.tensor_scalar(out=ang[:, :, j], in0=phi[:], scalar1=float(m), scalar2=None, op0=ALU.mult))(j, m), reads=[V], writes=[V])
                P.op('act', lambda e: e.activation(out=mag[:], in_=mag[:], func=AF.Exp), reads=[V], writes=[V])
                TWO_PI = 2.0 * math.pi
                MAGIC = 12582912.0

                def reduce_sin(dst, shift):
                    P.op('dve', lambda e: e.tensor_scalar(out=kk[:], in0=ang[:], scalar1=shift, scalar2=1.0 / TWO_PI, op0=ALU.add, op1=ALU.mult), reads=[V], writes=[V])
                    P.op('dve', lambda e: e.tensor_scalar(out=kk[:], in0=kk[:], scalar1=MAGIC, scalar2=None, op0=ALU.add), reads=[V], writes=[V])
                    P.op('dve', lambda e: e.tensor_scalar(out=kk[:], in0=kk[:], scalar1=-MAGIC, scalar2=-TWO_PI, op0=ALU.add, op1=ALU.mult), reads=[V], writes=[V])
                    P.op('dve', lambda e: e.scalar_tensor_tensor(out=an2[:], in0=ang[:], scalar=shift, in1=kk[:], op0=ALU.add, op1=ALU.add), reads=[V], writes=[V])
                    P.op('dve', lambda e: e.tensor_scalar(out=an2[:], in0=an2[:], scalar1=-3.14159, scalar2=3.14159, op0=ALU.max, op1=ALU.min), reads=[V], writes=[V])
                    P.op('act', lambda e: e.activation(out=dst[:], in_=an2[:], func=AF.Sin), reads=[V], writes=[V])
                reduce_sin(ci, 0.0)
                reduce_sin(cr, math.pi / 2)
                P.op('dve', lambda e: e.tensor_tensor(out=cr[:], in0=cr[:], in1=mag[:], op=ALU.mult), reads=[V], writes=[V])
                P.op('dve', lambda e: e.tensor_tensor(out=ci[:], in0=ci[:], in1=mag[:], op=ALU.mult), reads=[V], writes=[V])
                P.op('dve', lambda e: e.tensor_scalar(out=t1[:], in0=cr[:, :, 1], scalar1=-1.0, scalar2=None, op0=ALU.add), reads=[V], writes=[V])
                P.op('dve', lambda e: e.tensor_tensor(out=den[:], in0=lr[:], in1=lr[:], op=ALU.mult), reads=[V], writes=[V])
                P.op('dve', lambda e: e.tensor_tensor(out=t2[:], in0=li[:], in1=li[:], op=ALU.mult), reads=[V], writes=[V])
                P.op('dve', lambda e: e.tensor_tensor(out=den[:], in0=den[:], in1=t2[:], op=ALU.add), reads=[V], writes=[V])
                P.op('dve', lambda e: e.reciprocal(out=den[:], in_=den[:]), reads=[V], writes=[V])
                P.op('dve', lambda e: e.tensor_tensor(out=fr[:], in0=t1[:], in1=lr[:], op=ALU.mult), reads=[V], writes=[V])
                P.op('dve', lambda e: e.tensor_tensor(out=t2[:], in0=ci[:, :, 1], in1=li[:], op=ALU.mult), reads=[V], writes=[V])
                P.op('dve', lambda e: e.tensor_tensor(out=fr[:], in0=fr[:], in1=t2[:], op=ALU.add), reads=[V], writes=[V])
                P.op('dve', lambda e: e.tensor_tensor(out=fr[:], in0=fr[:], in1=den[:], op=ALU.mult), reads=[V], writes=[V])
                P.op('dve', lambda e: e.tensor_tensor(out=fi[:], in0=ci[:, :, 1], in1=lr[:], op=ALU.mult), reads=[V], writes=[V])
                P.op('dve', lambda e: e.tensor_tensor(out=t2[:], in0=t1[:], in1=li[:], op=ALU.mult), reads=[V], writes=[V])
                P.op('dve', lambda e: e.tensor_tensor(out=fi[:], in0=fi[:], in1=t2[:], op=ALU.subtract), reads=[V], writes=[V])
                P.op('dve', lambda e: e.tensor_tensor(out=fi[:], in0=fi[:], in1=den[:], op=ALU.mult), reads=[V], writes=[V])
                P.op('dve', lambda e: e.tensor_scalar(out=fi[:], in0=fi[:], scalar1=sg1[:, 0:1], scalar2=None, op0=ALU.mult), reads=[V, 'sg1'], writes=[V])
                P.op('dve', lambda e: e.tensor_scalar(out=cin[:], in0=ci[:], scalar1=nsg[:, 0:1], scalar2=None, op0=ALU.mult), reads=[V, 'nsg'], writes=[V])
                P.op('dve', lambda e: e.tensor_scalar(out=ci[:], in0=ci[:], scalar1=sg1[:, 0:1], scalar2=None, op0=ALU.mult), reads=[V, 'sg1'], writes=[V])
                for dgi in range(16):
                    P.op('dve', (lambda dgi: lambda e: e.tensor_scalar(out=bst[:, dgi, :], in0=b0[:, dgi, :], scalar1=fr[:, dgi:dgi + 1], scalar2=None, op0=ALU.mult))(dgi), reads=[V, 's5in'], writes=[V])
                    P.op('dve', (lambda dgi: lambda e: e.scalar_tensor_tensor(out=bst[:, dgi, :], in0=b1[:, dgi, :], scalar=fi[:, dgi:dgi + 1], in1=bst[:, dgi, :], op0=ALU.mult, op1=ALU.add))(dgi), reads=[V, 's5in'], writes=[V])
                P.op('dve', lambda e: e.tensor_scalar(out=c0[:], in0=c0[:], scalar1=nsg[:, 0:1], scalar2=None, op0=ALU.mult), reads=[V, 's5in', 'nsg'], writes=[V])
                P.op('pool', lambda e: e.memset(bpad[:], 0.0), writes=['bpad'])
                P.op('pool', lambda e: e.memset(epad[:], 0.0), writes=['epad'])
                for d in range(2):
                    P.op('pool', (lambda d: lambda e: e.memset(X[d][:], 0.0))(d), writes=[('X', d)])
                    P.op('pool', (lambda d: lambda e: e.memset(Xb[d][:], 0.0))(d), writes=[('Xb', d)])

                yi = 0
                for g in range(8):
                    G = 'grp'
                    for d in range(2):
                        dgi = d * 8 + g
                        for m in range(9):
                            P.op('dve', (lambda d, dgi, m: lambda e: e.tensor_scalar(out=rm[:, d, m, :], in0=idf[:], scalar1=cr[:, dgi, m:m + 1], scalar2=None, op0=ALU.mult))(d, dgi, m), reads=[V, 'idf', G], writes=[G])
                            P.op('dve', (lambda d, dgi, m: lambda e: e.scalar_tensor_tensor(out=rm[:, d, m, :], in0=jab[:], scalar=ci[:, dgi, m:m + 1], in1=rm[:, d, m, :], op0=ALU.mult, op1=ALU.add))(d, dgi, m), reads=[V, 'jab', G], writes=[G])
                        for m in range(8):
                            P.op('pool', (lambda d, dgi, m: lambda e: e.tensor_scalar(out=qm[:, d, m, :], in0=idf[:], scalar1=cr[:, dgi, m:m + 1], scalar2=None, op0=ALU.mult))(d, dgi, m), reads=[V, 'idf', G], writes=[G])
                            P.op('dve', (lambda d, dgi, m: lambda e: e.scalar_tensor_tensor(out=qm[:, d, m, :], in0=jab[:], scalar=cin[:, dgi, m:m + 1], in1=qm[:, d, m, :], op0=ALU.mult, op1=ALU.add))(d, dgi, m), reads=[V, 'jab', G], writes=[G])
                        for k in range(NLEV):
                            j = 8 if k == 0 else 8 + k
                            P.op('pool', (lambda d, dgi, k, j: lambda e: e.tensor_scalar(out=qlev[:, d, k, :], in0=idf[:], scalar1=cr[:, dgi, j:j + 1], scalar2=None, op0=ALU.mult))(d, dgi, k, j), reads=[V, 'idf', G], writes=[G])
                            P.op('dve', (lambda d, dgi, k, j: lambda e: e.scalar_tensor_tensor(out=qlev[:, d, k, :], in0=jab[:], scalar=cin[:, dgi, j:j + 1], in1=qlev[:, d, k, :], op0=ALU.mult, op1=ALU.add))(d, dgi, k, j), reads=[V, 'jab', G], writes=[G])
                        P.op('dve', (lambda d, dgi: lambda e: e.tensor_copy(out=bpad[:, d, 112:128], in_=bst[:, dgi, :]))(d, dgi), reads=[V, 'bpad', G], writes=[G])
                        pi = next_ps()

                        def fnE(e, d=d, dgi=dgi, pi=pi):
                            ins = None
                            for m in range(9):
                                ins = e.matmul(ps[pi][:, m * 16:(m + 1) * 16], lhsT=rm[:, d, m, :], rhs=c0[:, dgi, :], start=True, stop=True)
                            return ins
                        P.op('pe', fnE, reads=[G, V], writes=[psk[pi]])
                        P.op('act', (lambda d, pi: lambda e: e.activation(out=ebuf[:, d, :, :].rearrange("p a b -> p (a b)"), in_=ps[pi][:, 0:144], func=AF.Copy))(d, pi), reads=[psk[pi], G], writes=[G])
                        if d == 0:
                            P.op('dve', lambda e: e.tensor_copy(out=wY[:, 0, :], in_=ebuf[:, 0, 1:9, :].rearrange("p a b -> p (a b)")), reads=[G], writes=[G])
                            P.op('dve', lambda e: e.tensor_copy(out=epad[:, 0, 112:240], in_=ebuf[:, 0, 0:8, :].rearrange("p a b -> p (a b)")), reads=[G, 'epad'], writes=[G])
                        else:
                            for t in range(8):
                                P.op('dve', (lambda t: lambda e: e.tensor_copy(out=wY[:, 1, 16 * t:16 * t + 16], in_=ebuf[:, 1, 8 - t, :]))(t), reads=[G], writes=[G])
                                P.op('dve', (lambda t: lambda e: e.tensor_copy(out=epad[:, 1, 112 - 16 * t:128 - 16 * t], in_=ebuf[:, 1, t, :]))(t), reads=[G, 'epad'], writes=[G])
                        pi = next_ps()

                        def fnS(e, d=d, pi=pi):
                            ins = None
                            for t in range(8):
                                pw = 7 - t if d == 0 else t
                                ins = e.matmul(ps[pi][:, 0:128], lhsT=bpad[:, d, (7 - t) * 16:(7 - t) * 16 + 128], rhs=qm[:, d, pw, :], start=(t == 0), stop=(t == 7))
                            return ins
                        P.op('pe', fnS, reads=[G], writes=[psk[pi]])
                        P.op('act', (lambda d, pi: lambda e: e.activation(out=wS[:, d, :], in_=ps[pi][:, 0:128], func=AF.Copy))(d, pi), reads=[psk[pi], G], writes=[G])
                    pi = next_ps()

                    def fnK(e, pi=pi):
                        ins = None
                        n = 0
                        for d in range(2):
                            for t in range(8):
                                ins = e.matmul(ps[pi][:, 0:128], lhsT=bpad[:, d, (7 - t) * 16:(7 - t) * 16 + 128],
                                               rhs=epad[:, d, (7 - t) * 16:(7 - t) * 16 + 128], start=(n == 0), stop=(n == 15))
                                n += 1
                        return ins
                    P.op('pe', fnK, reads=[G], writes=[psk[pi]])
                    P.op('dve', (lambda g, pi: lambda e: e.scalar_tensor_tensor(out=wK[:], in0=idf[:], scalar=dsk[:, g:g + 1], in1=ps[pi][:, 0:128], op0=ALU.mult, op1=ALU.add))(g, pi),
                         reads=[psk[pi], 'idf', 's5in', G], writes=[G])
                    for d in range(2):
                        off = 1 if d == 0 else 0
                        for ct in range(2 * NC // CW):
                            pi = next_ps()
                            mm_group(pi, [(wS[:, d, :], ub[:, g, ct * CW:(ct + 1) * CW])], N=CW, extra_reads=[G, ('ub', g)])
                            P.op('act', (lambda d, pi, ct, off: lambda e: e.activation(out=X[d][:, off + ct * CW:off + (ct + 1) * CW], in_=ps[pi][:, 0:CW], func=AF.Copy))(d, pi, ct, off),
                                 reads=[psk[pi]], writes=[('X', d)])
                    NCH = 2 * NC
                    for k in range(NLEV):
                        s = 2 ** k
                        for d in range(2):
                            off = 1 if d == 0 else 0
                            n = NCH - s
                            pos = 0
                            while pos < n:
                                w = min(512, n - pos)
                                src0 = off + pos if d == 0 else off + pos + s
                                dst0 = off + pos + s if d == 0 else off + pos
                                pi = next_ps()
                                mm_group(pi, [(qlev[:, d, k, :], X[d][:, src0:src0 + w])], N=w, extra_reads=[G, ('X', d)])
                                P.op('dve', (lambda d, pi, dst0, w: lambda e: e.tensor_tensor(out=ps[pi][:, 0:w], in0=ps[pi][:, 0:w], in1=X[d][:, dst0:dst0 + w], op=ALU.add))(d, pi, dst0, w),
                                     reads=[psk[pi], ('X', d)], writes=[psk[pi], ('Xstage', d, pos)])
                                pos += w
                                P_pending.append((d, pi, dst0, w))
                            for (d_, pi_, dst0_, w_) in P_pending:
                                P.op('act', (lambda d_, pi_, dst0_, w_: lambda e: e.activation(out=X[d_][:, dst0_:dst0_ + w_], in_=ps[pi_][:, 0:w_], func=AF.Copy))(d_, pi_, dst0_, w_),
                                     reads=[psk[pi_]], writes=[('X', d_)])
                            del P_pending[:]
                    for d in range(2):
                        P.op('dve', (lambda d: lambda e: e.tensor_copy(out=Xb[d][:, :], in_=X[d][:, :]))(d), reads=[('X', d)], writes=[('Xb', d)])
                    for ct in range(2 * NC // CW):
                        pi = next_ps()
                        c0_ = ct * CW
                        mm_group(pi, [(wK[:], ub[:, g, c0_:c0_ + CW]),
                                      (wY[:, 0, :], Xb[0][:, c0_:c0_ + CW]),
                                      (wY[:, 1, :], Xb[1][:, c0_ + 1:c0_ + 1 + CW])], N=CW,
                                 extra_reads=[G, ('ub', g), ('Xb', 0), ('Xb', 1)])
                        P.op('act', (lambda pi, c0_, yi: lambda e: e.activation(out=yt[yi][:, c0_:c0_ + CW], in_=ps[pi][:, 0:CW], func=AF.Copy))(pi, c0_, yi),
                             reads=[psk[pi]], writes=[('yt', yi)])
                    for tau in range(8):
                        for r in range(2):
                            P.dma('sp', res_ap(1, 16 * g, 16, r * NT + tau * NC, NC), yt[yi][16 * tau:16 * tau + 16, r * NC:(r + 1) * NC],
                                  reads=[('yt', yi)], writes=[('res', 128, g, tau, r)])
                    yi ^= 1
                P.emit()

            with ExitStack() as ph:
                def sa(name, shape, dt):
                    return ph.enter_context(nc.sbuf_tensor(name + '_L%d' % l, list(shape), dt))
                tA = sa("tAa", [128, NT], BF16)
                tB = sa("tBa", [128, NT], BF16)
                Qs = [[sa("Qs%d_%d" % (i, c), [32, 512], BF16) for c in range(2)] for i in range(2)]
                qA = [sa("qA%d" % i, [32, 512], BF16) for i in range(2)]
                qB = [sa("qB%d" % i, [32, 512], BF16) for i in range(2)]
                Kh = [[sa("Kh%d_%d" % (i, c), [32, S2], BF16) for c in range(2)] for i in range(2)]
                Vx = sa("Vx", [128, KC, 2, 65], BF16)
                eT = [sa("eT%d" % i, [128, 512], BF16) for i in range(4)]
                lq = sa("lq", [128, 4, 32], F32)
                lt = sa("lt", [128, 2, 32], F32)
                lam = sa("lam", [128, 2], F32)
                nlam = sa("nlam", [128, 1], F32)
                gv = sa("gv", [128, 64], F32)
                oS = [sa("oS%d" % i, [65, 512], F32) for i in range(2)]
                rr = sa("rr", [128, 2, 4], F32)
                t2a = sa("t2a", [128, 64], F32)
                oo = sa("oo", [128, 4, 64], F32)
                sq = sa("sq", [128, 64], F32)
                ss = sa("ss", [128, 4], F32)
                yk = sa("yk", [128, 4, 64], F32)
                yo = sa("yo", [64, 512], BF16)
                P.dma('sp', lq[:], lqk.ap()[l], writes=['lq'])
                P.dma('sp', gv[:], sublng.ap()[l], writes=['gv'])
                P.op('dve', lambda e: e.tensor_tensor(out=lt[:, 0, :], in0=lq[:, 0, :], in1=lq[:, 1, :], op=ALU.mult), reads=['lq'], writes=['lt'])
                P.op('dve', lambda e: e.tensor_tensor(out=lt[:, 1, :], in0=lq[:, 2, :], in1=lq[:, 3, :], op=ALU.mult), reads=['lq'], writes=['lt'])
                P.op('dve', lambda e: e.tensor_reduce(out=lam[:], in_=lt[:], op=ALU.add, axis=mybir.AxisListType.X), reads=['lt'], writes=['lam'])
                P.op('act', lambda e: e.activation(out=lam[:], in_=lam[:], func=AF.Exp), reads=['lam'], writes=['lam'])
                P.op('dve', lambda e: e.tensor_tensor(out=nlam[:], in0=lam[:, 1:2], in1=lam[:, 0:1], op=ALU.subtract), reads=['lam'], writes=['nlam'])
                P.op('dve', lambda e: e.tensor_scalar(out=nlam[:], in0=nlam[:], scalar1=-lam_init, scalar2=None, op0=ALU.add), reads=['nlam'], writes=['nlam'])
                P.op('dve', lambda e: e.tensor_scalar(out=gv[:], in0=gv[:], scalar1=1.0 - lam_init, scalar2=None, op0=ALU.mult), reads=['gv'], writes=['gv'])
                P.op('pool', lambda e: e.memset(Vx[:], 1.0), writes=['Vx'])

                def sel(rows, gat_rows, nparts, out_ap, view, r, okey):
                    P.dma('sp', tA[0:nparts, :], loc.ap()[rows, :], reads=loc_keys_all, writes=['tA'])
                    P.dma('sp', tB[0:nparts, :], gat_rows, reads=GK, writes=['tB'])
                    P.op('pool', lambda e: e.tensor_scalar(out=out_ap, in0=view(tB[0:nparts, :]), scalar1=msk[0:nparts, 2 + r:3 + r], scalar2=None, op0=ALU.mult),
                         reads=['tB', 'msk'], writes=[okey])
                    P.op('dve', lambda e: e.scalar_tensor_tensor(out=out_ap, in0=view(tA[0:nparts, :]), scalar=msk[0:nparts, r:r + 1], in1=out_ap, op0=ALU.mult, op1=ALU.add),
                         reads=['tA', 'msk', okey], writes=[okey])
                for r in range(2):
                    for h2 in range(2):
                        for (T_, row0, key) in ((Kh, 512, 'K'),):
                            for c in range(2):
                                rb = row0 + 64 * h2 + 32 * c
                                rows = slice(rb, rb + 32)
                                sel(rows, gat_ap(r, rb, 32), 32, T_[h2][c][:, r * NT:(r + 1) * NT], lambda a: a, r, key)
                    vview = lambda t_: t_.ap()
                    lv = loc.ap()[640:768, :].rearrange("a (b c) -> (a b) c", c=128).rearrange("(i p) e -> p i e", p=128)
                    gvw = gat_ap(r, 640, 128).rearrange("a (b c) -> (a b) c", c=128).rearrange("(i p) e -> p i e", p=128)
                    tAv = tA[:, :].rearrange("p (i e) -> p i e", e=128)
                    tBv = tB[:, :].rearrange("p (i e) -> p i e", e=128)
                    P.dma('sp', tAv, lv, reads=loc_keys_all, writes=['tA'])
                    P.dma('sp', tBv, gvw, reads=GK, writes=['tB'])
                    ni = NT // 128
                    for h2 in range(2):
                        outv = Vx[:, r * ni:(r + 1) * ni, h2, 0:64]
                        P.op('pool', (lambda outv, h2, r: lambda e: e.tensor_scalar(out=outv, in0=tBv[:, :, h2 * 64:(h2 + 1) * 64], scalar1=msk[:, 2 + r:3 + r], scalar2=None, op0=ALU.mult))(outv, h2, r),
                             reads=['tB', 'msk'], writes=['Vx'])
                        P.op('dve', (lambda outv, h2, r: lambda e: e.scalar_tensor_tensor(out=outv, in0=tAv[:, :, h2 * 64:(h2 + 1) * 64], scalar=msk[:, r:r + 1], in1=outv, op0=ALU.mult, op1=ALU.add))(outv, h2, r),
                             reads=['tA', 'msk', 'Vx'], writes=['Vx'])
                P.emit()
                SC = 1.0 / math.sqrt(32.0)
                ei = 0
                for h2 in range(2):
                    for qt in range(S2 // 512):
                        qs = slice(qt * 512, (qt + 1) * 512)
                        rq = (qt * 512) // NT
                        cq = qt * 512 - rq * NT
                        qb = qt % 2
                        for c in range(2):
                            rb = 384 + 64 * h2 + 32 * c
                            P.dma('sp', qA[c][:], loc.ap()[rb:rb + 32, cq:cq + 512], reads=loc_keys_all, writes=[('qA', c)])
                            P.dma('sp', qB[c][:], gat_ap(rq, rb, 32)[:, cq:cq + 512], reads=GK, writes=[('qB', c)])
                            P.op('pool', (lambda c, qb, rq: lambda e: e.tensor_scalar(out=Qs[qb][c][:], in0=qB[c][:], scalar1=msk[0:32, 2 + rq:3 + rq], scalar2=None, op0=ALU.mult))(c, qb, rq),
                                 reads=[('qB', c), 'msk'], writes=[('Qs', qb, c)])
                            P.op('dve', (lambda c, qb, rq: lambda e: e.scalar_tensor_tensor(out=Qs[qb][c][:], in0=qA[c][:], scalar=msk[0:32, rq:rq + 1], in1=Qs[qb][c][:], op0=ALU.mult, op1=ALU.add))(c, qb, rq),
                                 reads=[('qA', c), 'msk', ('Qs', qb, c)], writes=[('Qs', qb, c)])
                        acc = [next_ps(), next_ps()]
                        for kc in range(KC):
                            for c in range(2):
                                pi = next_ps()
                                while pi in acc:
                                    pi = next_ps()
                                mm_group(pi, [(Kh[h2][c][:, kc * 128:(kc + 1) * 128], Qs[qb][c][:, :])],
                                         extra_reads=[('Qs', qb, c), 'K'])
                                P.op('act', (lambda pi, ei: lambda e: e.activation(out=eT[ei][:], in_=ps[pi][:, :], func=AF.Exp, scale=SC))(pi, ei),
                                     reads=[psk[pi]], writes=[('eT', ei)])

                                def fnPV(e, kc=kc, c=c, ei=ei, h2=h2, a=acc[c]):
                                    return e.matmul(ps[a][0:65, :], lhsT=Vx[:, kc, h2, :], rhs=eT[ei][:], start=(kc == 0), stop=(kc == KC - 1))
                                P.op('pe', fnPV, reads=[('eT', ei), 'Vx'], writes=[psk[acc[c]]])
                                ei = (ei + 1) % 4
                        for c in range(2):
                            P.op('act', (lambda c, a: lambda e: e.activation(out=oS[c][:], in_=ps[a][0:65, :], func=AF.Copy))(c, acc[c]),
                                 reads=[psk[acc[c]]], writes=[('oS', c)])
                        tp = [acc[0], acc[1]]
                        for c in range(2):
                            def fnT(e, c=c, a=tp[c]):
                                ins = None
                                for i4 in range(4):
                                    ins = e.transpose(out=ps[a][:, i4 * 65:(i4 + 1) * 65], in_=oS[c][:, i4 * 128:(i4 + 1) * 128], identity=idf[0:65, 0:65])
                                return ins
                            P.op('pe', fnT, reads=[('oS', c), 'idf'], writes=[psk[tp[c]]])
                        o1 = ps[tp[0]][:, 0:260].rearrange("p (i e) -> p i e", e=65)
                        o2 = ps[tp[1]][:, 0:260].rearrange("p (i e) -> p i e", e=65)
                        EP = 'ep'
                        P.op('dve', lambda e, o1=o1: e.reciprocal(out=rr[:, 0, :], in_=o1[:, :, 64]), reads=[psk[tp[0]], EP], writes=[EP])
                        P.op('dve', lambda e, o2=o2: e.reciprocal(out=rr[:, 1, :], in_=o2[:, :, 64]), reads=[psk[tp[1]], EP], writes=[EP])
                        for i4 in range(4):
                            P.op('dve', (lambda i4, o2: lambda e: e.tensor_scalar(out=t2a[:], in0=o2[:, i4, 0:64], scalar1=rr[:, 1, i4:i4 + 1], scalar2=nlam[:, 0:1], op0=ALU.mult, op1=ALU.mult))(i4, o2),
                                 reads=[psk[tp[1]], EP, 'nlam'], writes=[EP])
                            P.op('dve', (lambda i4, o1: lambda e: e.scalar_tensor_tensor(out=oo[:, i4, :], in0=o1[:, i4, 0:64], scalar=rr[:, 0, i4:i4 + 1], in1=t2a[:], op0=ALU.mult, op1=ALU.add))(i4, o1),
                                 reads=[psk[tp[0]], EP], writes=[EP])
                            P.op('dve', (lambda i4: lambda e: e.tensor_tensor(out=sq[:], in0=oo[:, i4, :], in1=oo[:, i4, :], op=ALU.mult))(i4), reads=[EP], writes=[EP])
                            P.op('dve', (lambda i4: lambda e: e.tensor_reduce(out=ss[:, i4:i4 + 1], in_=sq[:], op=ALU.add, axis=mybir.AxisListType.X))(i4), reads=[EP], writes=[EP])
                        P.op('dve', lambda e: e.tensor_scalar(out=ss[:], in0=ss[:], scalar1=1.0 / 64.0, scalar2=LN_EPS, op0=ALU.mult, op1=ALU.add), reads=[EP], writes=[EP])
                        P.op('act', lambda e: e.activation(out=ss[:], in_=ss[:], func=AF.Sqrt), reads=[EP], writes=[EP])
                        P.op('dve', lambda e: e.reciprocal(out=ss[:], in_=ss[:]), reads=[EP], writes=[EP])
                        for i4 in range(4):
                            P.op('dve', (lambda i4: lambda e: e.scalar_tensor_tensor(out=yk[:, i4, :], in0=oo[:, i4, :], scalar=ss[:, i4:i4 + 1], in1=gv[:], op0=ALU.mult, op1=ALU.mult))(i4),
                                 reads=[EP, 'gv'], writes=[EP])
                        a = tp[0]

                        def fnT2(e, a=a):
                            ins = None
                            for i4 in range(4):
                                ins = e.transpose(out=ps[a][0:64, i4 * 128:(i4 + 1) * 128], in_=yk[:, i4, :], identity=idf[:, :])
                            return ins
                        P.op('pe', fnT2, reads=[EP, 'idf', psk[tp[0]]], writes=[psk[tp[0]]])
                        P.op('act', (lambda a: lambda e: e.activation(out=yo[:], in_=ps[a][0:64, 0:512], func=AF.Copy))(a), reads=[psk[a], 'yo'], writes=['yo'])
                        P.dma('sp', res_ap(3, 64 * h2, 64, qt * 512, 512), yo[:], reads=['yo'], writes=[('res', 384, h2, qt)])
                P.emit()
            res_keys = [k for k in list(P.lastw.keys()) if isinstance(k, tuple) and k[0] == 'res']
            for mixi in range(4):
                for j in range(RSPLIT):
                    P.allgather(res_p[mixi][j].ap().opt(), gat2_p[mixi][j].ap().opt(), reads=res_keys, writes=[('gat2', mixi, j)])
            G2K = [('gat2', mixi, j) for mixi in range(4) for j in range(RSPLIT)]

            if debug and l == 0:
                P.dma('sp', dbg_loc.ap(), loc.ap(), reads=loc_keys_all, writes=['dbg1'])
                P.dma('sp', dbg_bgd.ap(), bgd.ap(), reads=[('bgd', hh, tt) for hh in range(2) for tt in range(NTT)], writes=['dbg2'])
                for mixi in range(4):
                    for j in range(RSPLIT):
                        P.dma('sp', dbg_res.ap()[mixi * 128:(mixi + 1) * 128, j * RW:(j + 1) * RW], res_p[mixi][j].ap(), reads=res_keys, writes=[('dbg3', mixi, j)])
            with ExitStack() as ph:
                def sa(name, shape, dt):
                    return ph.enter_context(nc.sbuf_tensor(name + '_L%d' % l, list(shape), dt))
                yb = sa("yb", [128, 8, NT], BF16)
                sub = ExitStack()
                tA = sub.enter_context(nc.sbuf_tensor("tAc_L%d" % l, [128, NT], BF16))
                tB = sub.enter_context(nc.sbuf_tensor("tBc_L%d" % l, [128, NT], BF16))
                for mix in range(4):
                    rows = slice(mix * 128, (mix + 1) * 128)
                    P.dma('sp', tA[:], res_ap(mix, 0, 128, 0, NT), reads=res_keys, writes=['tA'])
                    P.dma('sp', tB[:], res_ap(mix, 0, 128, NT, NT), reads=res_keys, writes=['tB'])
                    P.op('pool', (lambda mix: lambda e: e.tensor_scalar(out=yb[:, 2 * mix, :], in0=tB[:], scalar1=msk[:, 1:2], scalar2=None, op0=ALU.mult))(mix), reads=['tB', 'msk'], writes=[('yb', 2 * mix)])
                    P.op('dve', (lambda mix: lambda e: e.scalar_tensor_tensor(out=yb[:, 2 * mix, :], in0=tA[:], scalar=msk[:, 0:1], in1=yb[:, 2 * mix, :], op0=ALU.mult, op1=ALU.add))(mix), reads=['tA', 'msk', ('yb', 2 * mix)], writes=[('yb', 2 * mix)])
                    P.dma('sp', tA[:], gat2_ap(mix, 1, 0, NT), reads=G2K, writes=['tA'])
                    P.dma('sp', tB[:], gat2_ap(mix, 0, NT, NT), reads=G2K, writes=['tB'])
                    P.op('pool', (lambda mix: lambda e: e.tensor_scalar(out=yb[:, 2 * mix + 1, :], in0=tB[:], scalar1=msk[:, 1:2], scalar2=None, op0=ALU.mult))(mix), reads=['tB', 'msk'], writes=[('yb', 2 * mix + 1)])
                    P.op('dve', (lambda mix: lambda e: e.scalar_tensor_tensor(out=yb[:, 2 * mix + 1, :], in0=tA[:], scalar=msk[:, 0:1], in1=yb[:, 2 * mix + 1, :], op0=ALU.mult, op1=ALU.add))(mix), reads=['tA', 'msk', ('yb', 2 * mix + 1)], writes=[('yb', 2 * mix + 1)])
                P.emit()
                sub.close()
                cg_ = sa("cg_", [128, 2], F32)
                cb_ = sa("cb_", [128, 2], F32)
                bgl = sa("bgl", [128, 2], F32)
                g1 = sa("g1", [128, 8], F32)
                b1_ = sa("b1_", [128, 8], F32)
                g2 = sa("g2", [128, 8], F32)
                b2_ = sa("b2_", [128, 8], F32)
                wg = sa("wg", [128, 2, 256], BF16)
                wo = sa("wo", [128, 8, D], BF16)
                for (t_, src) in ((cg_, clng), (cb_, clnb), (bgl, bglu), (g1, ln1g), (b1_, ln1b), (g2, ln2g), (b2_, ln2b)):
                    P.dma('sp', t_[:], src.ap()[l], writes=['vecs'])
                P.dma('pool', wg[:], wglu.ap()[l].rearrange("(a p) e -> p a e", p=128), writes=['wg'])
                for ec in range(8):
                    P.dma('pool', wo[:, ec, :], w_out.ap()[l][ec * 128:(ec + 1) * 128, :], writes=[('wo', ec)])
                vv = sa("vv", [128, 8, TW], F32)
                xt = sa("xt", [128, 8, TW], F32)
                sqv = sa("sqv", [128, TW], F32)
                mean = sa("mean", [128, TW], F32)
                rstd = sa("rstd", [128, TW], F32)
                tmp = sa("tmp", [128, TW], F32)
                tmp2 = sa("tmp2", [128, TW], F32)
                bgt = sa("bgt", [128, 2, TW], BF16)
                xo = sa("xo", [128, TW], F32)

                def ln_stats(nch, vkeys):
                    pm = next_ps()
                    mm_group(pm, [(onesf[:], vv[:, i, :]) for i in range(nch)], extra_reads=['onesf'] + vkeys)
                    pq = next_ps()
                    for i in range(nch):
                        P.op('act', (lambda i: lambda e: e.activation(out=sqv[:], in_=vv[:, i, :], func=AF.Square))(i), reads=[vkeys[i], 'sqv'], writes=['sqv'])

                        def fnq(e, i=i, pq=pq):
                            return e.matmul(ps[pq][:, :], lhsT=onesf[:], rhs=sqv[:], start=(i == 0), stop=(i == nch - 1))
                        P.op('pe', fnq, reads=['sqv', 'onesf'], writes=[psk[pq]])
                    dinv = 1.0 / (128.0 * nch)
                    P.op('act', lambda e: e.activation(out=mean[:], in_=ps[pm][:, :], func=AF.Copy, scale=dinv), reads=[psk[pm]], writes=['mean'])
                    P.op('dve', lambda e: e.tensor_tensor(out=tmp[:], in0=mean[:], in1=mean[:], op=ALU.mult), reads=['mean'], writes=['tmp'])
                    P.op('dve', lambda e: e.scalar_tensor_tensor(out=tmp[:], in0=ps[pq][:, :], scalar=dinv, in1=tmp[:], op0=ALU.mult, op1=ALU.subtract), reads=[psk[pq], 'tmp'], writes=['tmp'])
                    P.op('dve', lambda e: e.tensor_scalar(out=tmp[:], in0=tmp[:], scalar1=LN_EPS, scalar2=None, op0=ALU.add), reads=['tmp'], writes=['tmp'])
                    P.op('act', lambda e: e.activation(out=rstd[:], in_=tmp[:], func=AF.Sqrt), reads=['tmp'], writes=['rstd'])
                    P.op('dve', lambda e: e.reciprocal(out=rstd[:], in_=rstd[:]), reads=['rstd'], writes=['rstd'])

                def ln_apply(i, gcol, bcol, outs, vkey):
                    P.op('dve', lambda e: e.tensor_tensor(out=tmp2[:], in0=vv[:, i, :], in1=mean[:], op=ALU.subtract), reads=[vkey, 'mean'], writes=['tmp2'])
                    P.op('dve', lambda e: e.tensor_tensor(out=tmp2[:], in0=tmp2[:], in1=rstd[:], op=ALU.mult), reads=['tmp2', 'rstd'], writes=['tmp2'])
                    for (ap_, key) in outs:
                        P.op('pool', (lambda ap_: lambda e: e.tensor_scalar(out=ap_, in0=tmp2[:], scalar1=gcol, scalar2=bcol, op0=ALU.mult, op1=ALU.add))(ap_), reads=['tmp2', 'vecs'], writes=[key])

                def do_tile(tt):
                    sl = slice(tt * TW, (tt + 1) * TW)
                    for i in range(2):
                        P.op('dve', (lambda i: lambda e: e.tensor_copy(out=vv[:, i, :], in_=yb[:, i, sl]))(i), reads=[('yb', i)], writes=[('vv', i)])
                    ln_stats(2, [('vv', 0), ('vv', 1)])
                    for i in range(2):
                        ln_apply(i, cg_[:, i:i + 1], cb_[:, i:i + 1], [(xo[:], 'xo')], ('vv', i))
                        P.op('act', (lambda i: lambda e: e.activation(out=yb[:, i, sl], in_=xo[:], func=AF.Silu))(i), reads=['xo'], writes=[('yb', i)])
                    for i in range(2):
                        s_ = 2 + i
                        P.op('dve', (lambda s_, i: lambda e: e.tensor_tensor(out=vv[:, i, :], in0=yb[:, s_, sl], in1=yb[:, s_, sl], op=ALU.mult))(s_, i), reads=[('yb', s_)], writes=[('vv', i)])
                        P.op('dve', (lambda i: lambda e: e.tensor_scalar(out=vv[:, i, :], in0=vv[:, i, :], scalar1=0.044715, scalar2=1.0, op0=ALU.mult, op1=ALU.add))(i), reads=[('vv', i)], writes=[('vv', i)])
                        P.op('dve', (lambda s_, i: lambda e: e.tensor_tensor(out=vv[:, i, :], in0=vv[:, i, :], in1=yb[:, s_, sl], op=ALU.mult))(s_, i), reads=[('vv', i), ('yb', s_)], writes=[('vv', i)])
                        P.op('act', (lambda i: lambda e: e.activation(out=vv[:, i, :], in_=vv[:, i, :], func=AF.Sigmoid, scale=2.0 * math.sqrt(2.0 / math.pi)))(i), reads=[('vv', i)], writes=[('vv', i)])
                        P.op('dve', (lambda s_, i: lambda e: e.tensor_tensor(out=yb[:, s_, sl], in0=vv[:, i, :], in1=yb[:, s_, sl], op=ALU.mult))(s_, i), reads=[('vv', i), ('yb', s_)], writes=[('yb', s_)])
                    for i in range(2):
                        pi = next_ps()
                        mm_group(pi, [(wg[:, a, i * 128:(i + 1) * 128], yb[:, 2 + a, sl]) for a in range(2)], extra_reads=['wg', ('yb', 2), ('yb', 3)])
                        P.op('act', (lambda i, pi: lambda e: e.activation(out=vv[:, 2 + i, :], in_=ps[pi][:, :], func=AF.Sigmoid, bias=bgl[:, i:i + 1]))(i, pi), reads=[psk[pi], 'vecs'], writes=[('vv', 2 + i)])
                    for i in range(2):
                        P.op('dve', (lambda i: lambda e: e.tensor_tensor(out=yb[:, 2 + i, sl], in0=vv[:, 2 + i, :], in1=yb[:, 2 + i, sl], op=ALU.mult))(i), reads=[('vv', 2 + i), ('yb', 2 + i)], writes=[('yb', 2 + i)])
                    P.dma('sp', bgt[:], bgd.ap()[:, sl].rearrange("(a p) t -> p a t", p=128), reads=[('bgd', 0, tt), ('bgd', 1, tt)], writes=['bgt'])
                    for i in range(2):
                        P.op('pool', (lambda i: lambda e: e.tensor_tensor(out=yb[:, 4 + i, sl], in0=yb[:, 4 + i, sl], in1=bgt[:, i, :], op=ALU.mult))(i), reads=['bgt', ('yb', 4 + i)], writes=[('yb', 4 + i)])
                    P.dma('sp', xt[:], xres.ap()[:, sl].rearrange("(a p) t -> p a t", p=128), reads=['xres', ('xres', tt)], writes=['xt'])
                    for dcn in range(8):
                        pi = next_ps()
                        mm_group(pi, [(wo[:, ec, dcn * 128:(dcn + 1) * 128], yb[:, ec, sl]) for ec in range(8)],
                                 extra_reads=[('wo', ec) for ec in range(8)] + [('yb', ec) for ec in range(8)])
                        P.op('dve', (lambda dcn, pi: lambda e: e.scalar_tensor_tensor(out=vv[:, dcn, :], in0=xt[:, dcn, :], scalar=ALPHA, in1=ps[pi][:, :], op0=ALU.mult, op1=ALU.add))(dcn, pi),
                             reads=[psk[pi], 'xt'], writes=[('vv', dcn)])
                    ln_stats(8, [('vv', i) for i in range(8)])
                    for dcn in range(8):
                        ln_apply(dcn, g1[:, dcn:dcn + 1], b1_[:, dcn:dcn + 1], [(xt[:, dcn, :], 'xt'), (xb[:, dcn, sl], ('xb', dcn))], ('vv', dcn))
                    P.dma('sp', xres.ap()[:, sl].rearrange("(a p) t -> p a t", p=128), xt[:], reads=['xt'], writes=[('xres', tt)])
                for tt in range(NTT):
                    do_tile(tt)
                P.emit()

            with ExitStack() as ph:
                def sa(name, shape, dt):
                    return ph.enter_context(nc.sbuf_tensor(name + '_L%d' % l, list(shape), dt))
                TT = min(1024, NT)
                NST = NT // TT
                NT2 = TT // TW
                gT = sa("gT", [128, NFC, TT], BF16)
                wa = [sa("wa%d" % i, [128, 8, 128], BF16) for i in range(2)]
                wc = [sa("wc%d" % i, [128, 8, 128], BF16) for i in range(2)]
                w2b = [sa("w2b%d" % i, [128, NFC, 128], BF16) for i in range(2)]
                sl_ = [sa("slu%d" % i, [128, TW], F32) for i in range(2)]
                vv = sa("vvf", [128, 8, TW], F32)
                xt = sa("xtf", [128, 8, TW], F32)
                sqv = sa("sqvf", [128, TW], F32)
                mean = sa("meanf", [128, TW], F32)
                rstd = sa("rstdf", [128, TW], F32)
                tmp = sa("tmpf", [128, TW], F32)
                tmp2 = sa("tmp2f", [128, TW], F32)
                g2 = sa("g2f", [128, 8], F32)
                b2_ = sa("b2f", [128, 8], F32)
                P.dma('sp', g2[:], ln2g.ap()[l], writes=['vecs2'])
                P.dma('sp', b2_[:], ln2b.ap()[l], writes=['vecs2'])
                w1l = w1.ap()[l].rearrange("(dc p) f -> p dc f", p=128)
                w3l = w3.ap()[l].rearrange("(dc p) f -> p dc f", p=128)
                w2l = w2.ap()[l].rearrange("(fc p) d -> p fc d", p=128)
                last = (l == L - 1)
                wi2 = 0
                for stt in range(NST):
                    for fc in range(NFC):
                        i = wi2
                        wi2 ^= 1
                        P.dma('pool', wa[i][:], w1l[:, :, fc * 128:(fc + 1) * 128], writes=[('wa', i)])
                        P.dma('pool', wc[i][:], w3l[:, :, fc * 128:(fc + 1) * 128], writes=[('wc', i)])
                        for t2_ in range(NT2):
                            c0_ = stt * TT + t2_ * TW
                            pa = next_ps()
                            mm_group(pa, [(wa[i][:, dc, :], xb[:, dc, c0_:c0_ + TW]) for dc in range(8)], extra_reads=[('wa', i)] + [('xb', dc) for dc in range(8)])
                            pb = next_ps()
                            mm_group(pb, [(wc[i][:, dc, :], xb[:, dc, c0_:c0_ + TW]) for dc in range(8)], extra_reads=[('wc', i)] + [('xb', dc) for dc in range(8)])
                            si = t2_ % 2
                            P.op('act', (lambda pa, si: lambda e: e.activation(out=sl_[si][:], in_=ps[pa][:, :], func=AF.Silu))(pa, si), reads=[psk[pa]], writes=[('slu', si)])
                            P.op('dve', (lambda pb, si, fc, t2_: lambda e: e.tensor_tensor(out=gT[:, fc, t2_ * TW:(t2_ + 1) * TW], in0=ps[pb][:, :], in1=sl_[si][:], op=ALU.mult))(pb, si, fc, t2_),
                                 reads=[psk[pb], ('slu', si)], writes=[('gT', fc)])
                    for t2_ in range(NT2):
                        tt = stt * NT2 + t2_
                        sl = slice(tt * TW, (tt + 1) * TW)
                        P.dma('sp', xt[:], xres.ap()[:, sl].rearrange("(a p) t -> p a t", p=128), reads=[('xres', tt)], writes=['xtf'])
                        for dcn in range(8):
                            i = (t2_ * 8 + dcn) % 2
                            P.dma('pool', w2b[i][:], w2l[:, :, dcn * 128:(dcn + 1) * 128], writes=[('w2b', i)])
                            pi = next_ps()
                            mm_group(pi, [(w2b[i][:, fc, :], gT[:, fc, t2_ * TW:(t2_ + 1) * TW]) for fc in range(NFC)],
                                     extra_reads=[('w2b', i)] + [('gT', fc) for fc in range(NFC)])
                            P.op('dve', (lambda dcn, pi: lambda e: e.scalar_tensor_tensor(out=vv[:, dcn, :], in0=xt[:, dcn, :], scalar=ALPHA, in1=ps[pi][:, :], op0=ALU.mult, op1=ALU.add))(dcn, pi),
                                 reads=[psk[pi], 'xtf'], writes=[('vvf', dcn)])
                        pm = next_ps()
                        mm_group(pm, [(onesf[:], vv[:, i, :]) for i in range(8)], extra_reads=['onesf'] + [('vvf', i) for i in range(8)])
                        pq = next_ps()
                        for i in range(8):
                            P.op('act', (lambda i: lambda e: e.activation(out=sqv[:], in_=vv[:, i, :], func=AF.Square))(i), reads=[('vvf', i), 'sqvf'], writes=['sqvf'])

                            def fnq(e, i=i, pq=pq):
                                return e.matmul(ps[pq][:, :], lhsT=onesf[:], rhs=sqv[:], start=(i == 0), stop=(i == 7))
                            P.op('pe', fnq, reads=['sqvf', 'onesf'], writes=[psk[pq]])
                        dinv = 1.0 / 1024.0
                        P.op('act', (lambda pm: lambda e: e.activation(out=mean[:], in_=ps[pm][:, :], func=AF.Copy, scale=dinv))(pm), reads=[psk[pm]], writes=['meanf'])
                        P.op('dve', lambda e: e.tensor_tensor(out=tmp[:], in0=mean[:], in1=mean[:], op=ALU.mult), reads=['meanf'], writes=['tmpf'])
                        P.op('dve', (lambda pq: lambda e: e.scalar_tensor_tensor(out=tmp[:], in0=ps[pq][:, :], scalar=dinv, in1=tmp[:], op0=ALU.mult, op1=ALU.subtract))(pq), reads=[psk[pq], 'tmpf'], writes=['tmpf'])
                        P.op('dve', lambda e: e.tensor_scalar(out=tmp[:], in0=tmp[:], scalar1=LN_EPS, scalar2=None, op0=ALU.add), reads=['tmpf'], writes=['tmpf'])
                        P.op('act', lambda e: e.activation(out=rstd[:], in_=tmp[:], func=AF.Sqrt), reads=['tmpf'], writes=['rstdf'])
                        P.op('dve', lambda e: e.reciprocal(out=rstd[:], in_=rstd[:]), reads=['rstdf'], writes=['rstdf'])
                        for dcn in range(8):
                            P.op('dve', (lambda dcn: lambda e: e.tensor_tensor(out=tmp2[:], in0=vv[:, dcn, :], in1=mean[:], op=ALU.subtract))(dcn), reads=[('vvf', dcn), 'meanf', 'tmp2f'], writes=['tmp2f'])
                            P.op('dve', lambda e: e.tensor_tensor(out=tmp2[:], in0=tmp2[:], in1=rstd[:], op=ALU.mult), reads=['tmp2f', 'rstdf'], writes=['tmp2f'])
                            P.op('pool', (lambda dcn: lambda e: e.tensor_scalar(out=xt[:, dcn, :], in0=tmp2[:], scalar1=g2[:, dcn:dcn + 1], scalar2=b2_[:, dcn:dcn + 1], op0=ALU.mult, op1=ALU.add))(dcn), reads=['tmp2f', 'vecs2', 'xtf'], writes=['xtf'])
                            if not last:
                                P.op('pool', (lambda dcn, sl: lambda e: e.tensor_copy(out=xb[:, dcn, sl], in_=xt[:, dcn, :]))(dcn, sl), reads=['xtf'], writes=[('xbn', dcn, tt)])
                        if last:
                            P.dma('sp', yT.ap()[:, sl].rearrange("(a p) t -> p a t", p=128), xt[:], reads=['xtf'], writes=[('yT', tt)])
                        else:
                            P.dma('sp', xres.ap()[:, sl].rearrange("(a p) t -> p a t", p=128), xt[:], reads=['xtf'], writes=[('xres', tt)])
                for dc in range(8):
                    P.lastw[('xb', dc)] = P.lastw.get(('xbn', dc, NTT - 1), P.lastw.get(('xb', dc)))
                P.emit()
        P.final_wait([('yT', tt) for tt in range(NT // 512)])
    return nc


P_pending = []


_CACHE = {}


def _blockperm(nblk, h):
    idx = []
    for b in range(nblk):
        idx += list(range(b * 256 + h * 128, b * 256 + h * 128 + 128))
        idx += list(range(b * 256 + (1 - h) * 128, b * 256 + (1 - h) * 128 + 128))
    return np.array(idx)


def kernel(**inp):
    x = np.asarray(inp["x"], np.float32)
    B, S, _ = x.shape
    L = inp["w_in"].shape[0]
    NT = S // 2
    NC = NT // 8
    import os as _os
    dbg = bool(_os.environ.get("KDEBUG"))
    key = (NC, L, dbg)
    if key not in _CACHE:
        _CACHE[key] = build(NC, L, debug=dbg)
    nc = _CACHE[key]
    f = lambda k: np.asarray(inp[k], np.float32)
    tokperm = np.array([8 * c + t for t in range(8) for c in range(NC)])
    ident = np.eye(128, dtype=np.float32)
    jabs = np.zeros((128, 128), np.float32)
    jabs[np.arange(64), 64 + np.arange(64)] = 1.0
    jabs[64 + np.arange(64), np.arange(64)] = 1.0
    sgn = np.concatenate([-np.ones(64), np.ones(64)]).astype(np.float32)[:, None]
    per_h = {}
    for h in range(2):
        d = {}
        p9 = _blockperm(9, h)
        p4 = _blockperm(4, h)
        p1 = _blockperm(1, h)
        d["w_in"] = np.ascontiguousarray(f("w_in")[:, :, p9])
        d["w_out"] = np.ascontiguousarray(f("w_out")[:, p4, :])
        d["w1"] = f("w_ffn1")
        d["w3"] = f("w_ffn3")
        d["w2"] = f("w_ffn2")
        hs = slice(h * 128, (h + 1) * 128)
        d["cdw"] = np.ascontiguousarray(f("conf_dw_w")[:, :, hs].transpose(0, 2, 1))
        d["cdb"] = np.ascontiguousarray(f("conf_dw_b")[:, hs, None])
        d["scw"] = np.ascontiguousarray(f("sc_conv_w")[:, :, hs].transpose(0, 2, 1))
        d["clng"] = np.ascontiguousarray(f("conf_ln_g")[:, p1].reshape(L, 2, 128).transpose(0, 2, 1))
        d["clnb"] = np.ascontiguousarray(f("conf_ln_b")[:, p1].reshape(L, 2, 128).transpose(0, 2, 1))
        gs = slice(h * 8, (h + 1) * 8)

        def dgn(a):
            t = a[:, :, gs, :].reshape(L, 16, 64).transpose(0, 2, 1)
            return np.ascontiguousarray(np.concatenate([t, t], axis=1))
        d["s5lr"] = dgn(f("s5_a_re"))
        d["s5li"] = dgn(f("s5_a_im"))
        ls = f("s5_log_step")[:, :, gs].reshape(L, 1, 16)
        d["s5ls"] = np.ascontiguousarray(np.broadcast_to(ls, (L, 128, 16)))
        br = f("s5_b_re")[:, :, gs].reshape(L, 16, 64, 16).transpose(0, 2, 1, 3)
        bim = f("s5_b_im")[:, :, gs].reshape(L, 16, 64, 16).transpose(0, 2, 1, 3)
        d["s5b0"] = np.ascontiguousarray(np.concatenate([br, bim], axis=1))
        d["s5b1"] = np.ascontiguousarray(np.concatenate([bim, br], axis=1))
        cr = f("s5_c_re")[:, :, gs].reshape(L, 16, 16, 64).transpose(0, 3, 1, 2)
        cim = f("s5_c_im")[:, :, gs].reshape(L, 16, 16, 64).transpose(0, 3, 1, 2)
        d["s5c0"] = np.ascontiguousarray(np.concatenate([cr, cim], axis=1))
        dsk = f("s5_d")[:, hs].reshape(L, 8, 16)
        d["s5d"] = np.ascontiguousarray(np.broadcast_to(dsk.transpose(0, 2, 1)[:, None, :, :], (L, 8, 16, 8)).reshape(L, 128, 8))
        d["wglu"] = np.ascontiguousarray(f("s5_w_glu")[:, p1][:, :, p1])
        d["bglu"] = np.ascontiguousarray(f("s5_b_glu")[:, p1].reshape(L, 2, 128).transpose(0, 2, 1))
        lq = np.stack([f("da_lq1"), f("da_lk1"), f("da_lq2"), f("da_lk2")], axis=1)
        d["lqk"] = np.ascontiguousarray(np.broadcast_to(lq[:, None], (L, 128, 4, 32)))
        d["sublng"] = np.ascontiguousarray(np.broadcast_to(f("da_subln_g")[:, None, :], (L, 128, 64)))
        for nm, src in (("ln1g", "ln1_g"), ("ln1b", "ln1_b"), ("ln2g", "ln2_g"), ("ln2b", "ln2_b")):
            d[nm] = np.ascontiguousarray(f(src).reshape(L, 8, 128).transpose(0, 2, 1))
        m = np.zeros((128, 4), np.float32)
        m[:, 0] = 1.0 if h == 0 else 0.0
        m[:, 1] = 1.0 if h == 1 else 0.0
        m[:, 2] = 1.0 - m[:, 0]
        m[:, 3] = 1.0 - m[:, 1]
        d["masks"] = m
        pos = (h * NT + tokperm).astype(np.float32)
        inv_freq = (ROPE_THETA ** (-np.arange(0, 8, 2, dtype=np.float32) / 8.0)).astype(np.float32)
        ang = pos[None, :] * inv_freq[:, None]
        cosT = np.ones((128, NT), np.float32)
        sinT = np.zeros((128, NT), np.float32)
        for blk in range(4):
            for dd in range(8):
                cosT[blk * 32 + dd] = np.cos(ang[dd % 4])
                sinT[blk * 32 + dd] = np.sin(ang[dd % 4])
        d["cosT"] = cosT
        d["sinT"] = sinT
        d["ident"] = ident
        d["jabs"] = jabs
        d["sgn"] = sgn
        per_h[h] = d
    in_maps = []
    for core in range(8):
        b, h = core // 2, core % 2
        m = dict(per_h[h])
        m["xT"] = np.ascontiguousarray(x[b, h * NT + tokperm, :].T)
        in_maps.append(m)
    r = run_bass_kernel_spmd(nc, in_maps, core_ids=list(range(8)))
    if dbg:
        global DEBUG_OUT
        DEBUG_OUT = [{k: np.asarray(r.results[c][k]) for k in ("dbg_loc", "dbg_res", "dbg_bgd")} for c in range(8)]
    out = np.empty((B, S, D), np.float32)
    for core in range(8):
        b, h = core // 2, core % 2
        out[b, h * NT + tokperm, :] = np.asarray(r.results[core]["yT"], np.float32).T
    return out
```
